# Optimizing a Trainium2 kernel written in Bass

```python
import jax, jax.numpy as jnp
from jax import lax
import numpy as np

D_MODEL = 2048
BATCH = 2
SEQ = 4096
DEPTH = 2

PLE_DIM = 256
MIX_WIDTH = D_MODEL
NSA_HEADS = 16
NSA_KV_HEADS = 4
NSA_WIDTH = MIX_WIDTH // 2
NSA_HEAD_DIM = NSA_WIDTH // NSA_HEADS
NSA_KV_WIDTH = NSA_KV_HEADS * NSA_HEAD_DIM
CMP_LEN = 32
CMP_STRIDE = 16
SLC_BLOCK = 64
N_SELECT = 16
WINDOW = 512
Q_BLOCK = 128
LRU_WIDTH = MIX_WIDTH // 4
LRU_BLOCKS = 8
LRU_BLOCK_DIM = LRU_WIDTH // LRU_BLOCKS
CONV_WIDTH = 4
LRU_C = 8.0
GLA_WIDTH = MIX_WIDTH // 4
GLA_HEADS = 4
GLA_HEAD_DIM = GLA_WIDTH // GLA_HEADS
GLA_GATE_RANK = 16
GLA_GATE_TAU = 16.0
GLA_CHUNK = 64
D_FF = 4 * D_MODEL
EPS = 1e-6
NEG_INF = -1e30

IN_SIZES = (NSA_WIDTH, 6 * NSA_KV_WIDTH, 3 * NSA_HEADS,
            LRU_WIDTH, LRU_WIDTH,
            GLA_WIDTH, GLA_WIDTH, GLA_WIDTH, GLA_WIDTH, GLA_GATE_RANK)
D_IN = NSA_WIDTH + 6 * NSA_KV_WIDTH + 3 * NSA_HEADS + 2 * LRU_WIDTH + 4 * GLA_WIDTH + GLA_GATE_RANK

kernel_name = 'hybrid_nsa_rglru_gla_trunk'


def rms_norm(x, g):
    x32 = x.astype(jnp.float32)
    y = x32 * lax.rsqrt(jnp.mean(x32 * x32, axis=-1, keepdims=True) + EPS)
    return (y * g.astype(jnp.float32)).astype(x.dtype)


def alibi_slopes(n):
    return jnp.exp2(-8.0 * jnp.arange(1, n + 1, dtype=jnp.float32) / n)


def nsa_mixer(q, kv, gates, cmp_w, cmp_pe):
    B, T, _ = q.shape
    G, HPG, dk = NSA_KV_HEADS, NSA_HEADS // NSA_KV_HEADS, NSA_HEAD_DIM
    q = q.reshape(B, T, G, HPG, dk) * dk ** -0.5
    kc, vc, ks, vs, kw, vw = [a.reshape(B, T, G, dk) for a in jnp.split(kv, 6, axis=-1)]
    slopes = alibi_slopes(NSA_HEADS).reshape(G, HPG)
    pos = jnp.arange(T)

    n_cmp = (T - CMP_LEN) // CMP_STRIDE + 1
    cmp_start = jnp.arange(n_cmp) * CMP_STRIDE
    cmp_idx = cmp_start[:, None] + jnp.arange(CMP_LEN)[None, :]

    def compress(a, w, pe):
        blk = a[:, cmp_idx] + pe[None, None, :, None, :]
        blk = blk.transpose(0, 1, 3, 2, 4).reshape(B, n_cmp, G, CMP_LEN * dk)
        return blk @ w

    k_cmp = compress(kc, cmp_w[0], cmp_pe[0])
    v_cmp = compress(vc, cmp_w[1], cmp_pe[1])
    d_cmp = (pos[:, None] - (cmp_start + CMP_LEN - 1)[None, :]).astype(jnp.float32)
    valid_cmp = d_cmp >= 0
    s = jnp.einsum('btghd,bngd->bghtn', q, k_cmp).astype(jnp.float32) - slopes[:, :, None, None] * d_cmp
    s = jnp.where(valid_cmp, s, NEG_INF)
    p_cmp = jax.nn.softmax(s, axis=-1) * valid_cmp
    o_cmp = jnp.einsum('bghtn,bngd->btghd', p_cmp, v_cmp)

    n_slc = T // SLC_BLOCK
    n_sel = min(N_SELECT, n_slc)
    slc_start = jnp.arange(n_slc) * SLC_BLOCK
    overlap = ((cmp_start[:, None] < slc_start[None, :] + SLC_BLOCK)
               & (slc_start[None, :] < cmp_start[:, None] + CMP_LEN)).astype(jnp.float32)
    imp = jnp.einsum('bghtn,nj->bgtj', p_cmp, overlap)
    blk = jnp.arange(n_slc)[None, :]
    t_blk = (pos // SLC_BLOCK)[:, None]
    forced = (blk == 0) | (blk == t_blk) | (blk == t_blk - 1)
    score = jnp.where(blk > t_blk, -1.0, jnp.where(forced, 1e4, imp))
    _, sel_idx = lax.top_k(score, n_sel)

    ks_blk = ks.reshape(B, n_slc, SLC_BLOCK, G, dk).transpose(0, 3, 1, 2, 4)
    vs_blk = vs.reshape(B, n_slc, SLC_BLOCK, G, dk).transpose(0, 3, 1, 2, 4)
    kw_pad = jnp.pad(kw, ((0, 0), (WINDOW, 0), (0, 0), (0, 0)))
    vw_pad = jnp.pad(vw, ((0, 0), (WINDOW, 0), (0, 0), (0, 0)))

    n_qb = T // Q_BLOCK
    q_blocks = q.reshape(B, n_qb, Q_BLOCK, G, HPG, dk).transpose(1, 0, 2, 3, 4, 5)
    idx_blocks = sel_idx.reshape(B, G, n_qb, Q_BLOCK, n_sel).transpose(2, 0, 1, 3, 4)
    t_blocks = pos.reshape(n_qb, Q_BLOCK)
    b_ix = jnp.arange(B)[:, None, None, None]
    g_ix = jnp.arange(G)[None, :, None, None]
    blk_off = jnp.arange(SLC_BLOCK)
    win_off = jnp.arange(WINDOW + Q_BLOCK)

    def block_fn(args):
        q_b, idx_b, t_b = args
        k_sel = ks_blk[b_ix, g_ix, idx_b].reshape(B, G, Q_BLOCK, n_sel * SLC_BLOCK, dk)
        v_sel = vs_blk[b_ix, g_ix, idx_b].reshape(B, G, Q_BLOCK, n_sel * SLC_BLOCK, dk)
        k_pos = (idx_b[..., None] * SLC_BLOCK + blk_off).reshape(B, G, Q_BLOCK, n_sel * SLC_BLOCK)
        d_sel = (t_b[None, None, :, None] - k_pos).astype(jnp.float32)[:, :, None]
        s1 = jnp.einsum('bqghd,bgqkd->bghqk', q_b, k_sel).astype(jnp.float32) - slopes[None, :, :, None, None] * d_sel
        s1 = jnp.where(d_sel >= 0, s1, NEG_INF)
        o_sel = jnp.einsum('bghqk,bgqkd->bqghd', jax.nn.softmax(s1, axis=-1), v_sel)
        w_idx = t_b[0] + win_off
        k_win = kw_pad[:, w_idx]
        v_win = vw_pad[:, w_idx]
        d_win = (t_b[:, None] - (w_idx - WINDOW)[None, :]).astype(jnp.float32)
        valid = (d_win >= 0) & (d_win < WINDOW) & (w_idx >= WINDOW)[None, :]
        s2 = jnp.einsum('bqghd,bkgd->bghqk', q_b, k_win).astype(jnp.float32) - slopes[:, :, None, None] * d_win
        s2 = jnp.where(valid, s2, NEG_INF)
        o_win = jnp.einsum('bghqk,bkgd->bqghd', jax.nn.softmax(s2, axis=-1), v_win)
        return o_sel, o_win

    o_sel, o_win = lax.map(block_fn, (q_blocks, idx_blocks, t_blocks))
    o_sel = o_sel.transpose(1, 0, 2, 3, 4, 5).reshape(B, T, G, HPG, dk)
    o_win = o_win.transpose(1, 0, 2, 3, 4, 5).reshape(B, T, G, HPG, dk)
    g = jax.nn.sigmoid(gates.astype(jnp.float32)).reshape(B, T, G, HPG, 3)
    o = g[..., 0:1] * o_cmp + g[..., 1:2] * o_sel + g[..., 2:3] * o_win
    return o.reshape(B, T, NSA_WIDTH)


def _lin_combine(e1, e2):
    a1, b1 = e1
    a2, b2 = e2
    return a1 * a2, a2 * b1 + b2


def rglru_mixer(xb, yb, conv_w, conv_b, wa, ba, wi, bi, lam):
    B, T, _ = xb.shape
    xp = jnp.pad(xb, ((0, 0), (CONV_WIDTH - 1, 0), (0, 0)))
    xc = conv_b + sum(xp[:, k:k + T] * conv_w[k] for k in range(CONV_WIDTH))
    xr = xc.reshape(B, T, LRU_BLOCKS, LRU_BLOCK_DIM)
    r = jax.nn.sigmoid(jnp.einsum('btnc,ncd->btnd', xr, wa).reshape(B, T, LRU_WIDTH) + ba)
    i = jax.nn.sigmoid(jnp.einsum('btnc,ncd->btnd', xr, wi).reshape(B, T, LRU_WIDTH) + bi)
    log_a = -LRU_C * r.astype(jnp.float32) * jax.nn.softplus(-lam.astype(jnp.float32))
    a = jnp.exp(log_a)
    u = jnp.sqrt(-jnp.expm1(2.0 * log_a)) * (i * xc).astype(jnp.float32)
    _, h = lax.associative_scan(_lin_combine, (a, u), axis=1)
    return h * jax.nn.gelu(yb.astype(jnp.float32))


def gla_mixer(q, k, v, r, z_lr, w_gate2, b_gate, norm_g):
    B, T, _ = q.shape
    H, d, C = GLA_HEADS, GLA_HEAD_DIM, GLA_CHUNK
    N = T // C
    log_alpha = jax.nn.log_sigmoid((z_lr @ w_gate2 + b_gate).astype(jnp.float32)) / GLA_GATE_TAU

    def chunks(a):
        return a.astype(jnp.float32).reshape(B, N, C, H, d)

    qc = chunks(q) * d ** -0.5
    kc = chunks(k)
    vc = chunks(v)
    bcum = jnp.cumsum(chunks(log_alpha), axis=2)
    b_last = bcum[:, :, -1:]
    q_t = qc * jnp.exp(bcum)
    k_t = kc * jnp.exp(-bcum)
    k_end = kc * jnp.exp(b_last - bcum)
    causal = jnp.tril(jnp.ones((C, C), dtype=bool))
    att = jnp.where(causal, jnp.einsum('bnthd,bnshd->bnhts', q_t, k_t), 0.0)
    o_intra = jnp.einsum('bnhts,bnshv->bnthv', att, vc)
    upd = jnp.einsum('bnshk,bnshv->nbhkv', k_end, vc)
    decay = jnp.exp(b_last[:, :, 0]).transpose(1, 0, 2, 3)

    def step(state, xs):
        d_n, u_n = xs
        return d_n[..., None] * state + u_n, state

    _, s_in = lax.scan(step, jnp.zeros((B, H, d, d), jnp.float32), (decay, upd))
    o_inter = jnp.einsum('bnthk,nbhkv->bnthv', q_t, s_in)
    o = rms_norm((o_intra + o_inter).reshape(B, T, H, d), norm_g.reshape(H, d))
    return o.reshape(B, T, GLA_WIDTH) * jax.nn.silu(r.astype(jnp.float32))


def hybrid_layer(h, p_i, norm_mix_pre, w_in, nsa_cmp_w, nsa_cmp_pe, lru_conv_w, lru_conv_b,
                 lru_wa, lru_ba, lru_wi, lru_bi, lru_lambda, gla_w_gate2, gla_b_gate, gla_norm,
                 w_out, norm_mix_post, norm_mlp_pre, w_up, w_down, norm_mlp_post, w_ple_gate, w_ple):
    u = rms_norm(h, norm_mix_pre)
    proj = u @ w_in
    offsets = tuple(int(o) for o in np.cumsum(IN_SIZES)[:-1])
    (nsa_q, nsa_kv, nsa_g, lru_x, lru_y,
     gla_q, gla_k, gla_v, gla_r, gla_z) = jnp.split(proj, offsets, axis=-1)
    o_a = nsa_mixer(nsa_q, nsa_kv, nsa_g, nsa_cmp_w, nsa_cmp_pe)
    o_b = rglru_mixer(lru_x, lru_y, lru_conv_w, lru_conv_b, lru_wa, lru_ba, lru_wi, lru_bi, lru_lambda)
    o_c = gla_mixer(gla_q, gla_k, gla_v, gla_r, gla_z, gla_w_gate2, gla_b_gate, gla_norm)
    mix = jnp.concatenate([o_a.astype(h.dtype), o_b.astype(h.dtype), o_c.astype(h.dtype)], axis=-1)
    h = h + rms_norm(mix @ w_out, norm_mix_post)
    u = rms_norm(h, norm_mlp_pre)
    h = h + rms_norm(jnp.square(jax.nn.relu(u @ w_up)) @ w_down, norm_mlp_post)
    h = h + jax.nn.sigmoid(h @ w_ple_gate) * (p_i @ w_ple)
    return h


def setup_inputs(seed: int = 0) -> dict:
    key = jax.random.key(seed)
    ks = jax.random.split(key, 24)
    dk = NSA_HEAD_DIM
    nrm = jax.random.normal

    def gain(k):
        return 1.0 + 0.1 * nrm(k, (DEPTH, D_MODEL), jnp.float32)

    u = jax.random.uniform(ks[12], (DEPTH, LRU_WIDTH), jnp.float32, minval=0.9, maxval=0.999)
    s = u ** (1.0 / LRU_C)
    return {
        'x': nrm(ks[0], (BATCH, SEQ, D_MODEL), jnp.float32),
        'p': nrm(ks[1], (DEPTH, BATCH, SEQ, PLE_DIM), jnp.float32),
        'norm_mix_pre': gain(ks[2]),
        'w_in': nrm(ks[3], (DEPTH, D_MODEL, D_IN), jnp.float32) * D_MODEL ** -0.5,
        'nsa_cmp_w': nrm(ks[4], (DEPTH, 2, CMP_LEN * dk, dk), jnp.float32) * (CMP_LEN * dk) ** -0.5,
        'nsa_cmp_pe': 0.1 * nrm(ks[5], (DEPTH, 2, CMP_LEN, dk), jnp.float32),
        'lru_conv_w': nrm(ks[6], (DEPTH, CONV_WIDTH, LRU_WIDTH), jnp.float32) * CONV_WIDTH ** -0.5,
        'lru_conv_b': 0.01 * nrm(ks[7], (DEPTH, LRU_WIDTH), jnp.float32),
        'lru_wa': nrm(ks[8], (DEPTH, LRU_BLOCKS, LRU_BLOCK_DIM, LRU_BLOCK_DIM), jnp.float32) * LRU_BLOCK_DIM ** -0.5,
        'lru_ba': 0.01 * nrm(ks[9], (DEPTH, LRU_WIDTH), jnp.float32),
        'lru_wi': nrm(ks[10], (DEPTH, LRU_BLOCKS, LRU_BLOCK_DIM, LRU_BLOCK_DIM), jnp.float32) * LRU_BLOCK_DIM ** -0.5,
        'lru_bi': 0.01 * nrm(ks[11], (DEPTH, LRU_WIDTH), jnp.float32),
        'lru_lambda': jnp.log(s) - jnp.log1p(-s),
        'gla_w_gate2': nrm(ks[13], (DEPTH, GLA_GATE_RANK, GLA_WIDTH), jnp.float32) * GLA_GATE_RANK ** -0.5,
        'gla_b_gate': 0.01 * nrm(ks[14], (DEPTH, GLA_WIDTH), jnp.float32),
        'gla_norm': 1.0 + 0.1 * nrm(ks[15], (DEPTH, GLA_WIDTH), jnp.float32),
        'w_out': nrm(ks[16], (DEPTH, MIX_WIDTH, D_MODEL), jnp.float32) * MIX_WIDTH ** -0.5,
        'norm_mix_post': gain(ks[17]),
        'norm_mlp_pre': gain(ks[18]),
        'w_up': nrm(ks[19], (DEPTH, D_MODEL, D_FF), jnp.float32) * D_MODEL ** -0.5,
        'w_down': nrm(ks[20], (DEPTH, D_FF, D_MODEL), jnp.float32) * D_FF ** -0.5,
        'norm_mlp_post': gain(ks[21]),
        'w_ple_gate': nrm(ks[22], (DEPTH, D_MODEL, D_MODEL), jnp.float32) * D_MODEL ** -0.5,
        'w_ple': nrm(ks[23], (DEPTH, PLE_DIM, D_MODEL), jnp.float32) * PLE_DIM ** -0.5,
    }


def reference(x, p, norm_mix_pre, w_in, nsa_cmp_w, nsa_cmp_pe, lru_conv_w, lru_conv_b,
              lru_wa, lru_ba, lru_wi, lru_bi, lru_lambda, gla_w_gate2, gla_b_gate, gla_norm,
              w_out, norm_mix_post, norm_mlp_pre, w_up, w_down, norm_mlp_post, w_ple_gate, w_ple):
    h = x
    for i in range(DEPTH):
        h = hybrid_layer(h, p[i], norm_mix_pre[i], w_in[i], nsa_cmp_w[i], nsa_cmp_pe[i],
                         lru_conv_w[i], lru_conv_b[i], lru_wa[i], lru_ba[i], lru_wi[i], lru_bi[i],
                         lru_lambda[i], gla_w_gate2[i], gla_b_gate[i], gla_norm[i], w_out[i],
                         norm_mix_post[i], norm_mlp_pre[i], w_up[i], w_down[i], norm_mlp_post[i],
                         w_ple_gate[i], w_ple[i])
    return h
```

```python
import contextlib
import numpy as np
import ml_dtypes
import concourse.bass as bass
import concourse.mybir as mybir
from concourse.bass_utils import run_bass_kernel_spmd

F32 = mybir.dt.float32
BF16 = mybir.dt.bfloat16
I32 = mybir.dt.int32
AF = mybir.ActivationFunctionType
ALU = mybir.AluOpType
AX = mybir.AxisListType

D = 2048
T = 4096
DFF = 8192
EPS = 1e-6
NCORES = 8


class Sched:
    ENGS = ("pe", "act", "dve", "pool", "sp")

    def __init__(self, nc):
        self.nc = nc
        self.ops = []

    def add(self, eng, fn, reads=(), writes=(), semkey=None):
        self.ops.append((eng, fn, tuple(reads), tuple(writes), semkey))

    def pe(self, fn, reads=(), writes=()):
        self.add("pe", fn, reads, writes)

    def act(self, fn, reads=(), writes=()):
        self.add("act", fn, reads, writes)

    def dve(self, fn, reads=(), writes=()):
        self.add("dve", fn, reads, writes)

    def pool(self, fn, reads=(), writes=()):
        self.add("pool", fn, reads, writes)

    def dma(self, eng, fn, reads=(), writes=(), semkey=None):
        assert semkey is not None
        self.add(eng, fn, reads, writes, semkey)

    def emit(self, stack):
        nc = self.nc
        ops = self.ops
        n = len(ops)
        last_w = {}
        readers = {}
        last_dma = {}
        deps = [None] * n
        for i, (eng, fn, rd, wr, sk) in enumerate(ops):
            d = set()
            for k in rd:
                if k in last_w:
                    d.add(last_w[k])
            for k in wr:
                if k in last_w:
                    d.add(last_w[k])
                d.update(readers.get(k, ()))
            if sk is not None and sk in last_dma:
                d.add(last_dma[sk])
            d.discard(i)
            deps[i] = d
            for k in rd:
                readers.setdefault(k, []).append(i)
            for k in wr:
                last_w[k] = i
                readers[k] = []
            if sk is not None:
                last_dma[sk] = i
        eng_sem = {e: stack.enter_context(nc.semaphore("sem_" + e)) for e in self.ENGS}
        dma_sems = {}
        eng_cnt = {e: 0 for e in self.ENGS}
        dma_cnt = {}
        event = [None] * n
        for i, (eng, fn, rd, wr, sk) in enumerate(ops):
            if sk is not None:
                if sk not in dma_sems:
                    dma_sems[sk] = stack.enter_context(nc.semaphore("dsem_%d" % len(dma_sems)))
                    dma_cnt[sk] = 0
                dma_cnt[sk] += 16
                event[i] = (("d", sk), dma_cnt[sk])
            elif fn is not None:
                eng_cnt[eng] += 1
                event[i] = (("e", eng), eng_cnt[eng])

        def semof(key):
            return eng_sem[key[1]] if key[0] == "e" else dma_sems[key[1]]

        per_eng = {e: [] for e in self.ENGS}
        for i, op in enumerate(ops):
            per_eng[op[0]].append(i)

        def run_engine(engname, engine):
            seen = {}
            for i in per_eng[engname]:
                eng, fn, rd, wr, sk = ops[i]
                need = {}
                for dpi in deps[i]:
                    ev = event[dpi]
                    if ev is None:
                        continue
                    key, val = ev
                    if key == ("e", "pe") and engname == "pe":
                        continue
                    if need.get(key, 0) < val:
                        need[key] = val
                for key, val in need.items():
                    if seen.get(key, 0) >= val:
                        continue
                    engine.wait_ge(semof(key), val)
                    seen[key] = val
                if fn is None:
                    continue
                ins = fn(engine)
                key, val = event[i]
                ins.then_inc(semof(key), 16 if key[0] == "d" else 1)

        block = stack.enter_context(nc.Block())

        @block.tensor
        def _(e):
            run_engine("pe", e)

        @block.scalar
        def _(e):
            run_engine("act", e)

        @block.vector
        def _(e):
            run_engine("dve", e)

        @block.gpsimd
        def _(e):
            run_engine("pool", e)

        @block.sync
        def _(e):
            run_engine("sp", e)


def make_identity(S, nc, sb, name="ident"):
    identf = sb(name + "f", [128, 128], F32)
    ident = sb(name, [128, 128], BF16)
    S.dve(lambda e: e.memset(identf[:], 0.0), writes=[name + "f"])
    S.pool(lambda e: e.affine_select(out=identf[:], in_=identf[:], pattern=[[-1, 128]],
                                     compare_op=ALU.not_equal, fill=1.0, base=0, channel_multiplier=1),
           reads=[name + "f"], writes=[name + "f"])
    S.dve(lambda e: e.tensor_copy(out=ident[:], in_=identf[:]), reads=[name + "f"], writes=[name])
    return ident, identf


def rstd_ops(S, ss, rs, n, keys_in, key_out):
    S.dve(lambda e: e.tensor_scalar(out=rs, in0=ss, scalar1=1.0 / n, scalar2=EPS, op0=ALU.mult, op1=ALU.add),
          reads=keys_in, writes=[key_out])
    S.act(lambda e: e.activation(out=rs, in_=rs, func=AF.Sqrt), reads=[key_out], writes=[key_out])
    S.dve(lambda e: e.reciprocal(out=rs, in_=rs), reads=[key_out], writes=[key_out])


def build_B():
    nc = bass.Bass("TRN2", target_bir_lowering=False)
    NT = 1024
    mixT = nc.dram_tensor("mixT", [D, NT], BF16, kind="ExternalInput").ap()
    h_in = nc.dram_tensor("h_in", [NT, D], F32, kind="ExternalInput").ap()
    pT = nc.dram_tensor("pT", [256, NT], F32, kind="ExternalInput").ap()
    w_out = nc.dram_tensor("w_out", [D, D], F32, kind="ExternalInput").ap()
    w_up = nc.dram_tensor("w_up", [D, DFF], F32, kind="ExternalInput").ap()
    w_down = nc.dram_tensor("w_down", [DFF, D], F32, kind="ExternalInput").ap()
    w_gate = nc.dram_tensor("w_gate", [D, D], F32, kind="ExternalInput").ap()
    w_ple = nc.dram_tensor("w_ple", [256, D], F32, kind="ExternalInput").ap()
    gains = nc.dram_tensor("gains", [3, D], F32, kind="ExternalInput").ap()
    h_out = nc.dram_tensor("h_out", [NT, D], F32, kind="ExternalOutput").ap()

    with contextlib.ExitStack() as st:
        sb = lambda name, shape, dt: st.enter_context(nc.sbuf_tensor(name, shape, dt))
        ps = lambda name, shape, dt: st.enter_context(nc.psum_tensor(name, shape, dt))
        S = Sched(nc)
        ident, _ = make_identity(S, nc, sb)

        h_sb = sb("h_sb", [128, 4, D], F32)
        big = sb("big", [128, 32768], BF16)
        scr = sb("scr", [128, 16384], BF16)
        wbuf = [sb("wbuf%d" % i, [128, 16, 512], BF16) for i in range(2)]
        gA = sb("gA", [128, D], F32)
        gB = sb("gB", [128, D], F32)
        pT_sb = sb("pT_sb", [128, 2, 512], BF16)
        wple_sb = sb("wple_sb", [128, 2, D], BF16)
        sqtmp = [sb("sqtmp%d" % i, [128, 512], F32) for i in range(2)]
        junk = sb("junk", [128, D], BF16)
        stat = sb("stat", [128, 8], F32)

        bigW = big[:, :].rearrange("p (c n) -> p c n", c=16)
        aT = big[:, :].rearrange("p (c n) -> p c n", c=64)
        uT = scr[:, 0:8192].rearrange("p (c n) -> p c n", c=16)
        xs = scr[:, 8192:10240]
        ysb = scr[:, 10240:14336].bitcast(F32)
        y2 = scr[:, :].bitcast(F32).rearrange("p (t n) -> p t n", t=4)
        SCR = ["uT", "xs", "ysb"]

        py = ps("py", [128, D], F32)
        pTr = ps("pTr", [128, 16, 128], BF16)
        pU = [ps("pU%d" % i, [128, 512], F32) for i in range(2)]

        wcount = [0]

        def load_w(src_ap, dst=None, extra_writes=()):
            i = wcount[0] % 2
            wcount[0] += 1
            d = wbuf[i]
            S.dma("pool", lambda e: e.dma_start(out=d[:], in_=src_ap), writes=["wbuf%d" % i], semkey="wbuf%d" % i)
            return d, "wbuf%d" % i

        S.dma("pool", lambda e: e.dma_start(out=wple_sb[:], in_=w_ple.rearrange("(c p) n -> p c n", p=128)),
              writes=["wple"], semkey="wple")

        for grp in range(2):
            t0 = grp * 512
            S.dma("sp", lambda e, t0=t0: e.dma_start(out=h_sb[:], in_=h_in[t0:t0 + 512, :].rearrange("(t p) d -> p t d", p=128)),
                  writes=["h0", "h1", "h2", "h3"], semkey="h_sb")
            S.dma("sp", lambda e, t0=t0: e.dma_start(out=uT, in_=mixT[:, t0:t0 + 512].rearrange("(c p) t -> p c t", p=128)),
                  writes=["uT"], semkey="uT")
            S.dma("pool", lambda e, t0=t0: e.dma_start(out=pT_sb[:], in_=pT[:, t0:t0 + 512].rearrange("(c p) t -> p c t", p=128)),
                  writes=["pT"], semkey="pT")
            S.dma("sp", lambda e: e.dma_start(out=gA[:], in_=gains[0].partition_broadcast(128)), writes=["gA"], semkey="gA")
            S.dma("sp", lambda e: e.dma_start(out=gB[:], in_=gains[1].partition_broadcast(128)), writes=["gB"], semkey="gB")
            for q in range(4):
                S.dma("pool", lambda e, q=q: e.dma_start(out=bigW[:, :, q * 512:(q + 1) * 512],
                                                        in_=w_out[:, q * 512:(q + 1) * 512].rearrange("(c p) n -> p c n", p=128)),
                      writes=["big%d" % q], semkey="big%d" % q)
            for tt in range(4):
                hk = "h%d" % tt
                for cb in range(4):
                    for c in range(16):
                        S.pe(lambda e, tt=tt, cb=cb, c=c: e.matmul(py[:, cb * 512:(cb + 1) * 512], lhsT=uT[:, c, tt * 128:(tt + 1) * 128],
                                                                  rhs=bigW[:, c, cb * 512:(cb + 1) * 512], start=(c == 0), stop=(c == 15)),
                             reads=["uT", "big%d" % cb], writes=["py%d" % cb])
                PY = ["py%d" % i for i in range(4)]
                S.act(lambda e: e.activation(out=junk[:], in_=py[:], func=AF.Square, accum_out=stat[:, 0:1]),
                      reads=PY, writes=["junk", "st0"])
                rstd_ops(S, stat[:, 0:1], stat[:, 1:2], D, ["st0"], "st1")
                S.dve(lambda e: e.scalar_tensor_tensor(out=ysb, in0=py[:], scalar=stat[:, 1:2], in1=gA[:], op0=ALU.mult, op1=ALU.mult),
                      reads=PY + ["st1", "gA"], writes=["ysb"])
                S.dve(lambda e, tt=tt: e.tensor_tensor(out=h_sb[:, tt, :], in0=h_sb[:, tt, :], in1=ysb, op=ALU.add),
                      reads=["ysb", hk], writes=[hk])
            for tt in range(4):
                hk = "h%d" % tt
                S.act(lambda e, tt=tt: e.activation(out=junk[:], in_=h_sb[:, tt, :], func=AF.Square, accum_out=stat[:, 2:3]),
                      reads=[hk], writes=["junk", "st2"])
                rstd_ops(S, stat[:, 2:3], stat[:, 3:4], D, ["st2"], "st3")
                S.dve(lambda e, tt=tt: e.scalar_tensor_tensor(out=xs, in0=h_sb[:, tt, :], scalar=stat[:, 3:4], in1=gB[:], op0=ALU.mult, op1=ALU.mult),
                      reads=[hk, "st3", "gB"], writes=["xs"])
                for c in range(16):
                    S.pe(lambda e, c=c: e.transpose(out=pTr[:, c, :], in_=xs[:, c * 128:(c + 1) * 128], identity=ident[:]),
                         reads=["xs", "ident"], writes=["pTr"])
                S.act(lambda e, tt=tt: e.copy(out=uT[:, :, tt * 128:(tt + 1) * 128], in_=pTr[:]), reads=["pTr"], writes=["uT"])
            S.dma("sp", lambda e: e.dma_start(out=gA[:], in_=gains[2].partition_broadcast(128)), writes=["gA"], semkey="gA")
            for fb in range(16):
                wt, wk = load_w(w_up[:, fb * 512:(fb + 1) * 512].rearrange("(c p) n -> p c n", p=128))
                for fi in range(4):
                    j = (fb * 4 + fi) % 2
                    pu = pU[j]
                    for c in range(16):
                        S.pe(lambda e, wt=wt, fi=fi, c=c, pu=pu: e.matmul(pu[:], lhsT=wt[:, c, fi * 128:(fi + 1) * 128], rhs=uT[:, c, :],
                                                                         start=(c == 0), stop=(c == 15)),
                             reads=["uT", wk], writes=["pU%d" % j])
                    sq = sqtmp[j]
                    S.act(lambda e, pu=pu, sq=sq: e.activation(out=sq[:], in_=pu[:], func=AF.Square), reads=["pU%d" % j], writes=["sq%d" % j])
                    S.dve(lambda e, pu=pu, sq=sq, k=fb * 4 + fi: e.scalar_tensor_tensor(out=aT[:, k, :], in0=pu[:], scalar=0.0, in1=sq[:],
                                                                                       op0=ALU.is_gt, op1=ALU.mult),
                          reads=["pU%d" % j, "sq%d" % j], writes=["aT%d" % (fb // 4)])
            for cb in range(4):
                for fq in range(4):
                    wt, wk = load_w(w_down[fq * 2048:(fq + 1) * 2048, cb * 512:(cb + 1) * 512].rearrange("(c p) n -> p c n", p=128))
                    for tt in range(4):
                        for c in range(16):
                            S.pe(lambda e, wt=wt, tt=tt, c=c, fq=fq: e.matmul(py[:, tt * 512:(tt + 1) * 512], lhsT=aT[:, fq * 16 + c, tt * 128:(tt + 1) * 128],
                                                                               rhs=wt[:, c, :], start=(fq == 0 and c == 0), stop=(fq == 3 and c == 15)),
                                 reads=["aT%d" % fq, wk], writes=["py%d" % tt])
                for tt in range(4):
                    S.act(lambda e, tt=tt, cb=cb: e.copy(out=y2[:, tt, cb * 512:(cb + 1) * 512], in_=py[:, tt * 512:(tt + 1) * 512]),
                          reads=["py%d" % tt] + SCR, writes=SCR)
            for tt in range(4):
                hk = "h%d" % tt
                S.act(lambda e, tt=tt: e.activation(out=junk[:], in_=y2[:, tt, :], func=AF.Square, accum_out=stat[:, 4:5]),
                      reads=SCR, writes=["junk", "st4"])
                rstd_ops(S, stat[:, 4:5], stat[:, 5:6], D, ["st4"], "st5")
                S.dve(lambda e, tt=tt: e.scalar_tensor_tensor(out=y2[:, tt, :], in0=y2[:, tt, :], scalar=stat[:, 5:6], in1=gA[:], op0=ALU.mult, op1=ALU.mult),
                      reads=SCR + ["st5", "gA"], writes=SCR)
                S.dve(lambda e, tt=tt: e.tensor_tensor(out=h_sb[:, tt, :], in0=h_sb[:, tt, :], in1=y2[:, tt, :], op=ALU.add),
                      reads=SCR + [hk], writes=[hk])
            for q in range(4):
                S.dma("pool", lambda e, q=q: e.dma_start(out=bigW[:, :, q * 512:(q + 1) * 512],
                                                        in_=w_gate[:, q * 512:(q + 1) * 512].rearrange("(c p) n -> p c n", p=128)),
                      reads=["aT%d" % i for i in range(4)], writes=["big%d" % q] + ["aT%d" % i for i in range(4)], semkey="big%d" % q)
            for tt in range(4):
                hk = "h%d" % tt
                S.dve(lambda e, tt=tt: e.tensor_copy(out=xs, in_=h_sb[:, tt, :]), reads=[hk], writes=["xs"])
                for c in range(16):
                    S.pe(lambda e, c=c: e.transpose(out=pTr[:, c, :], in_=xs[:, c * 128:(c + 1) * 128], identity=ident[:]),
                         reads=["xs", "ident"], writes=["pTr"])
                S.act(lambda e, tt=tt: e.copy(out=uT[:, :, tt * 128:(tt + 1) * 128], in_=pTr[:]), reads=["pTr"], writes=["uT"])
            for tt in range(4):
                hk = "h%d" % tt
                for half in range(2):
                    for q in range(2):
                        cb = half * 2 + q
                        for c in range(16):
                            S.pe(lambda e, tt=tt, cb=cb, c=c, q=q: e.matmul(py[:, q * 512:(q + 1) * 512], lhsT=uT[:, c, tt * 128:(tt + 1) * 128],
                                                                           rhs=bigW[:, c, cb * 512:(cb + 1) * 512], start=(c == 0), stop=(c == 15)),
                                 reads=["uT", "big%d" % cb], writes=["py%d" % q])
                        for c in range(2):
                            S.pe(lambda e, tt=tt, cb=cb, c=c, q=q: e.matmul(py[:, (2 + q) * 512:(3 + q) * 512], lhsT=pT_sb[:, c, tt * 128:(tt + 1) * 128],
                                                                           rhs=wple_sb[:, c, cb * 512:(cb + 1) * 512], start=(c == 0), stop=(c == 1)),
                                 reads=["pT", "wple"], writes=["py%d" % (2 + q)])
                    S.act(lambda e: e.activation(out=ysb[:, 0:1024], in_=py[:, 0:1024], func=AF.Sigmoid), reads=["py0", "py1"], writes=["ysb"])
                    S.dve(lambda e: e.tensor_tensor(out=ysb[:, 0:1024], in0=ysb[:, 0:1024], in1=py[:, 1024:2048], op=ALU.mult),
                          reads=["ysb", "py2", "py3"], writes=["ysb"])
                    S.dve(lambda e, tt=tt, half=half: e.tensor_tensor(out=h_sb[:, tt, half * 1024:(half + 1) * 1024],
                                                                      in0=h_sb[:, tt, half * 1024:(half + 1) * 1024], in1=ysb[:, 0:1024], op=ALU.add),
                          reads=["ysb", hk], writes=[hk])
            S.dma("sp", lambda e, t0=t0: e.dma_start(out=h_out[t0:t0 + 512, :].rearrange("(t p) d -> p t d", p=128), in_=h_sb[:]),
                  reads=["h0", "h1", "h2", "h3"], writes=["hout"], semkey="hout")
        S.add("sp", None, reads=["hout"])
        S.emit(st)
    return nc


_CACHE = {}


def _get(name, builder):
    if name not in _CACHE:
        _CACHE[name] = builder()
    return _CACHE[name]


def run_B(mixT_full, h_full, p_l, w_out, gains, w_up, w_down, w_gate, w_ple):
    nc = _get("B", build_B)
    in_maps = []
    pT_full = np.ascontiguousarray(p_l.reshape(8192, 256).T)
    for k in range(NCORES):
        sl = slice(k * 1024, (k + 1) * 1024)
        in_maps.append({
            "mixT": np.ascontiguousarray(mixT_full[:, sl]),
            "h_in": np.ascontiguousarray(h_full[sl]),
            "pT": np.ascontiguousarray(pT_full[:, sl]),
            "w_out": w_out, "w_up": w_up, "w_down": w_down, "w_gate": w_gate, "w_ple": w_ple,
            "gains": gains,
        })
    res = run_bass_kernel_spmd(nc, in_maps, core_ids=list(range(NCORES)))
    return np.concatenate([r["h_out"] for r in res.results], axis=0)


class FrontEnd:
    def __init__(self, S, nc, sb, ps, h_b, g_pre, ident, nbuf=2):
        self.S = S
        self.h_b = h_b
        self.ident = ident
        self.nbuf = nbuf
        self.h_t = [sb("fe_h%d" % i, [128, D], F32) for i in range(2)]
        self.xs = [sb("fe_xs%d" % i, [128, D], BF16) for i in range(2)]
        self.g_bc = sb("fe_g", [128, D], F32)
        self.uT = [sb("fe_uT%d" % i, [128, 16, 512], BF16) for i in range(nbuf)]
        self.junk = sb("fe_junk", [128, D], BF16)
        self.stat = sb("fe_stat", [128, 4], F32)
        self.pTr = ps("fe_pTr", [128, 16, 128], BF16)
        g_bc = self.g_bc
        S.dma("sp", lambda e: e.dma_start(out=g_bc[:], in_=g_pre.partition_broadcast(128)), writes=["fe_g"], semkey="fe_g")

    def block(self, blk):
        S = self.S
        ub = blk % self.nbuf
        uT = self.uT[ub]
        uk = "fe_uT%d" % ub
        for tt in range(4):
            tile = blk * 4 + tt
            j = tile % 2
            h_t, xs = self.h_t[j], self.xs[j]
            hk, xk, s0, s1 = "fe_h%d" % j, "fe_xs%d" % j, "fe_s%da" % j, "fe_s%db" % j
            stat, junk, g_bc, pTr, ident, h_b = self.stat, self.junk, self.g_bc, self.pTr, self.ident, self.h_b
            S.dma("sp", lambda e, h_t=h_t, tile=tile: e.dma_start(out=h_t[:], in_=h_b[tile * 128:(tile + 1) * 128, :]),
                  writes=[hk], semkey=hk)
            S.act(lambda e, h_t=h_t, j=j: e.activation(out=junk[:], in_=h_t[:], func=AF.Square, accum_out=stat[:, 2 * j:2 * j + 1]),
                  reads=[hk], writes=["fe_junk", s0])
            rstd_ops(S, stat[:, 2 * j:2 * j + 1], stat[:, 2 * j + 1:2 * j + 2], D, [s0], s1)
            S.dve(lambda e, h_t=h_t, xs=xs, j=j: e.scalar_tensor_tensor(out=xs[:], in0=h_t[:], scalar=stat[:, 2 * j + 1:2 * j + 2], in1=g_bc[:],
                                                                      op0=ALU.mult, op1=ALU.mult),
                  reads=[hk, s1, "fe_g"], writes=[xk])
            for c in range(16):
                S.pe(lambda e, xs=xs, c=c: e.transpose(out=pTr[:, c, :], in_=xs[:, c * 128:(c + 1) * 128], identity=ident[:]),
                     reads=[xk, "ident"], writes=["fe_pTr"])
            S.act(lambda e, uT=uT, tt=tt: e.copy(out=uT[:, :, tt * 128:(tt + 1) * 128], in_=pTr[:]), reads=["fe_pTr"], writes=[uk])
        return uT, uk


def load_wA(S, sb, w_ap, ncols, name="wA"):
    wt = sb(name, [128, 16, ncols], BF16)
    step = 4
    for c0 in range(0, 16, step):
        S.dma("pool", lambda e, c0=c0: e.dma_start(out=wt[:, c0:c0 + step, :],
                                                   in_=w_ap[c0 * 128:(c0 + step) * 128, :].rearrange("(c p) n -> p c n", p=128)),
              writes=[name], semkey="%s_%d" % (name, c0))
    return wt


def build_A_lru():
    nc = bass.Bass("TRN2", target_bir_lowering=False)
    h_b = nc.dram_tensor("h_b", [T, D], F32, kind="ExternalInput").ap()
    g_pre = nc.dram_tensor("g_pre", [D], F32, kind="ExternalInput").ap()
    w_a = nc.dram_tensor("w_a", [D, 256], F32, kind="ExternalInput").ap()
    par = nc.dram_tensor("par", [128, 8], F32, kind="ExternalInput").ap()
    wa = nc.dram_tensor("wa", [2, 64, 64], F32, kind="ExternalInput").ap()
    wi = nc.dram_tensor("wi", [2, 64, 64], F32, kind="ExternalInput").ap()
    o_out = nc.dram_tensor("o_out", [128, T], BF16, kind="ExternalOutput").ap()
    with contextlib.ExitStack() as st:
        sb = lambda name, shape, dt: st.enter_context(nc.sbuf_tensor(name, shape, dt))
        ps = lambda name, shape, dt: st.enter_context(nc.psum_tensor(name, shape, dt))
        S = Sched(nc)
        ident, _ = make_identity(S, nc, sb)
        fe = FrontEnd(S, nc, sb, ps, h_b, g_pre, ident)
        wA = load_wA(S, sb, w_a, 256)
        par_sb = sb("par_sb", [128, 8], F32)
        c1 = sb("c1", [128, 1], F32)
        S.dma("sp", lambda e: e.dma_start(out=par_sb[:], in_=par), writes=["par"], semkey="par")
        wbd = [sb("wbd%d" % i, [128, 128], BF16) for i in range(2)]
        for i, wsrc in enumerate((wa, wi)):
            S.dve(lambda e, i=i: e.memset(wbd[i][:], 0.0), writes=["wbd%d" % i])
            for n in range(2):
                S.dma("pool", lambda e, i=i, n=n, wsrc=wsrc: e.dma_start(out=wbd[i][n * 64:(n + 1) * 64, n * 64:(n + 1) * 64], in_=wsrc[n]),
                      reads=[], writes=["wbd%d" % i], semkey="wbd%d_%d" % (i, n))
        S.act(lambda e: e.activation(out=c1[:], in_=par_sb[:, 7:8], func=AF.Exp, scale=-1.0), reads=["par"], writes=["c1"])
        S.act(lambda e: e.activation(out=c1[:], in_=c1[:], func=AF.Ln, bias=1.0), reads=["c1"], writes=["c1"])
        S.dve(lambda e: e.tensor_scalar(out=c1[:], in0=c1[:], scalar1=-8.0, scalar2=None, op0=ALU.mult), reads=["c1"], writes=["c1"])

        xT = sb("xT", [128, T + 4], F32)
        yT = sb("yT", [128, T], F32)
        xc = sb("xc", [128, T], F32)
        xcb = sb("xcb", [128, T], BF16)
        a_all = sb("a_all", [128, T], F32)
        u_all = sb("u_all", [128, T], F32)
        tmp = [sb("tmp%d" % i, [128, 512], F32) for i in range(3)]
        obf = sb("obf", [128, T], BF16)
        pX = [ps("pX%d" % i, [128, 512], F32) for i in range(4)]

        S.dve(lambda e: e.memset(xT[:, 0:4], 0.0), writes=["xT"])
        for blk in range(8):
            uT, uk = fe.block(blk)
            for which in range(2):
                pp = pX[which]
                for c in range(16):
                    S.pe(lambda e, uT=uT, c=c, which=which, pp=pp: e.matmul(pp[:], lhsT=wA[:, c, which * 128:(which + 1) * 128], rhs=uT[:, c, :],
                                                                          start=(c == 0), stop=(c == 15)),
                         reads=[uk, "wA"], writes=["pX%d" % which])
            S.act(lambda e, blk=blk: e.copy(out=xT[:, 4 + blk * 512:4 + (blk + 1) * 512], in_=pX[0][:]), reads=["pX0"], writes=["xT"])
            S.act(lambda e, blk=blk: e.copy(out=yT[:, blk * 512:(blk + 1) * 512], in_=pX[1][:]), reads=["pX1"], writes=["yT"])
        S.dve(lambda e: e.tensor_scalar(out=xc[:], in0=xT[:, 1:1 + T], scalar1=par_sb[:, 0:1], scalar2=par_sb[:, 4:5], op0=ALU.mult, op1=ALU.add),
              reads=["xT", "par"], writes=["xc"])
        for k in range(1, 4):
            S.dve(lambda e, k=k: e.scalar_tensor_tensor(out=xc[:], in0=xT[:, 1 + k:1 + k + T], scalar=par_sb[:, k:k + 1], in1=xc[:],
                                                        op0=ALU.mult, op1=ALU.add),
                  reads=["xT", "par", "xc"], writes=["xc"])
        S.act(lambda e: e.copy(out=xcb[:], in_=xc[:]), reads=["xc"], writes=["xcb"])
        for blk in range(8):
            sl = slice(blk * 512, (blk + 1) * 512)
            S.pe(lambda e, sl=sl: e.matmul(pX[2][:], lhsT=wbd[0][:], rhs=xcb[:, sl], start=True, stop=True), reads=["xcb", "wbd0"], writes=["pX2"])
            S.pe(lambda e, sl=sl: e.matmul(pX[3][:], lhsT=wbd[1][:], rhs=xcb[:, sl], start=True, stop=True), reads=["xcb", "wbd1"], writes=["pX3"])
            S.act(lambda e: e.activation(out=tmp[0][:], in_=pX[2][:], func=AF.Sigmoid, bias=par_sb[:, 5:6]), reads=["pX2", "par"], writes=["tmp0"])
            S.act(lambda e, sl=sl: e.activation(out=a_all[:, sl], in_=tmp[0][:], func=AF.Exp, scale=c1[:]), reads=["tmp0", "c1"], writes=["a_all"])
            S.act(lambda e: e.activation(out=tmp[1][:], in_=pX[3][:], func=AF.Sigmoid, bias=par_sb[:, 6:7]), reads=["pX3", "par"], writes=["tmp1"])
            S.dve(lambda e, sl=sl: e.tensor_tensor(out=tmp[2][:], in0=a_all[:, sl], in1=a_all[:, sl], op=ALU.mult), reads=["a_all"], writes=["tmp2"])
            S.act(lambda e: e.activation(out=tmp[2][:], in_=tmp[2][:], func=AF.Sqrt, scale=-1.0, bias=1.0), reads=["tmp2"], writes=["tmp2"])
            S.dve(lambda e, sl=sl: e.tensor_tensor(out=tmp[1][:], in0=tmp[1][:], in1=xc[:, sl], op=ALU.mult), reads=["tmp1", "xc"], writes=["tmp1"])
            S.dve(lambda e, sl=sl: e.tensor_tensor(out=u_all[:, sl], in0=tmp[1][:], in1=tmp[2][:], op=ALU.mult), reads=["tmp1", "tmp2"], writes=["u_all"])
        S.dve(lambda e: e.tensor_tensor_scan(out=xc[:], data0=a_all[:], data1=u_all[:], initial=0.0, op0=ALU.mult, op1=ALU.add),
              reads=["a_all", "u_all", "xc"], writes=["xc"])
        S.dve(lambda e: e.tensor_tensor(out=a_all[:], in0=yT[:], in1=yT[:], op=ALU.mult), reads=["yT", "a_all"], writes=["a_all"])
        S.dve(lambda e: e.tensor_scalar(out=a_all[:], in0=a_all[:], scalar1=0.044715, scalar2=1.0, op0=ALU.mult, op1=ALU.add), reads=["a_all"], writes=["a_all"])
        S.dve(lambda e: e.tensor_tensor(out=a_all[:], in0=a_all[:], in1=yT[:], op=ALU.mult), reads=["a_all", "yT"], writes=["a_all"])
        S.act(lambda e: e.activation(out=a_all[:], in_=a_all[:], func=AF.Sigmoid, scale=1.5957691216), reads=["a_all"], writes=["a_all"])
        S.dve(lambda e: e.tensor_tensor(out=a_all[:], in0=a_all[:], in1=yT[:], op=ALU.mult), reads=["a_all", "yT"], writes=["a_all"])
        S.dve(lambda e: e.tensor_tensor(out=obf[:], in0=a_all[:], in1=xc[:], op=ALU.mult), reads=["a_all", "xc"], writes=["obf"])
        S.dma("sp", lambda e: e.dma_start(out=o_out, in_=obf[:]), reads=["obf"], writes=["o_out"], semkey="o_out")
        S.add("sp", None, reads=["o_out"])
        S.emit(st)
    return nc


def gcols_lru(g):
    base = 1024 + 6 * 256 + 48
    return np.concatenate([np.arange(base + g * 128, base + (g + 1) * 128), np.arange(base + 512 + g * 128, base + 512 + (g + 1) * 128)])


def run_A_lru(h_full, g_pre, w_in, conv_w, conv_b, lwa, lba, lwi, lbi, lam):
    nc = _get("A_lru", build_A_lru)
    in_maps = []
    for k in range(NCORES):
        b, g = k // 4, k % 4
        ch = slice(g * 128, (g + 1) * 128)
        par = np.stack([conv_w[0, ch], conv_w[1, ch], conv_w[2, ch], conv_w[3, ch], conv_b[ch], lba[ch], lbi[ch], lam[ch]], axis=1)
        in_maps.append({"h_b": np.ascontiguousarray(h_full[b]), "g_pre": g_pre,
                        "w_a": np.ascontiguousarray(w_in[:, gcols_lru(g)]), "par": np.ascontiguousarray(par.astype(np.float32)),
                        "wa": np.ascontiguousarray(lwa[2 * g:2 * g + 2]), "wi": np.ascontiguousarray(lwi[2 * g:2 * g + 2])})
    res = run_bass_kernel_spmd(nc, in_maps, core_ids=list(range(NCORES)))
    out = np.zeros((512, 2 * T), dtype=ml_dtypes.bfloat16)
    for k in range(NCORES):
        b, g = k // 4, k % 4
        out[g * 128:(g + 1) * 128, b * T:(b + 1) * T] = res.results[k]["o_out"]
    return out


def build_A_gla():
    nc = bass.Bass("TRN2", target_bir_lowering=False)
    h_b = nc.dram_tensor("h_b", [T, D], F32, kind="ExternalInput").ap()
    g_pre = nc.dram_tensor("g_pre", [D], F32, kind="ExternalInput").ap()
    w_a = nc.dram_tensor("w_a", [D, 528], F32, kind="ExternalInput").ap()
    par = nc.dram_tensor("par", [128, 2], F32, kind="ExternalInput").ap()
    wg2_d = nc.dram_tensor("wg2", [16, 128], F32, kind="ExternalInput").ap()
    o_out = nc.dram_tensor("o_out", [128, T], BF16, kind="ExternalOutput").ap()
    with contextlib.ExitStack() as st:
        sb = lambda name, shape, dt: st.enter_context(nc.sbuf_tensor(name, shape, dt))
        ps = lambda name, shape, dt: st.enter_context(nc.psum_tensor(name, shape, dt))
        S = Sched(nc)
        ident, _ = make_identity(S, nc, sb)
        fe = FrontEnd(S, nc, sb, ps, h_b, g_pre, ident, nbuf=1)
        wA = load_wA(S, sb, w_a, 528)
        bk = [ps("bk%d" % i, [128, 512], F32) for i in range(6)]
        par_sb = sb("par_sb", [128, 2], F32)
        negb = sb("negb", [128, 1], F32)
        wg2 = sb("wg2_sb", [16, 128], BF16)
        S.dma("sp", lambda e: e.dma_start(out=par_sb[:], in_=par), writes=["par"], semkey="par")
        S.dma("pool", lambda e: e.dma_start(out=wg2[:], in_=wg2_d), writes=["wg2"], semkey="wg2")
        S.dve(lambda e: e.tensor_scalar(out=negb[:], in0=par_sb[:, 0:1], scalar1=-1.0, scalar2=None, op0=ALU.mult), reads=["par"], writes=["negb"])

        qT = sb("qT", [128, T], BF16)
        kT = sb("kT", [128, T], BF16)
        srT = sb("srT", [128, T], BF16)
        zT = sb("zT", [16, T], BF16)
        v = sb("v", [128, 32, 128], BF16)
        l_all = sb("l_all", [128, T], F32)
        bc = sb("bc", [128, T], F32)
        mk = sb("mk", [128, T], F32)
        qtT = sb("qtT", [128, T], BF16)
        ktT = sb("ktT", [128, T], BF16)
        kendT = sb("kendT", [128, T], BF16)
        kend_tm = sb("kend_tm", [128, 32, 128], BF16)
        oT_all = sb("oT_all", [128, T], BF16)
        dec = sb("dec", [128, 64], F32)
        tmpe = sb("tmpe", [128, 512], F32)
        maskf = sb("maskf", [128, 128], F32)
        S_f = sb("S_f", [128, 128], F32)
        S_bf = [sb("S_bf%d" % i, [128, 128], BF16) for i in range(4)]
        qtA = [sb("qtA%d" % i, [128, 128], BF16) for i in range(2)]
        qtB = [sb("qtB%d" % i, [128, 128], BF16) for i in range(2)]
        attm = [sb("attm%d" % i, [128, 128], BF16) for i in range(2)]
        onb = [sb("onb%d" % i, [128, 128], BF16) for i in range(2)]
        gst = sb("gst", [128, 4], F32)
        gjunk = sb("gjunk", [128, 128], BF16)

        S.dve(lambda e: e.memset(mk[:], 1.0), writes=["mk"])
        S.dve(lambda e: e.memset(mk[:, :].rearrange("p (n c) -> p n c", c=64)[:, :, 0:1], 0.0), reads=["mk"], writes=["mk"])
        S.dve(lambda e: e.memset(maskf[:], 1.0), writes=["maskf"])
        S.pool(lambda e: e.affine_select(out=maskf[:], in_=maskf[:], pattern=[[1, 128]], compare_op=ALU.is_ge, fill=0.0,
                                         base=0, channel_multiplier=-1), reads=["maskf"], writes=["maskf"])
        S.dve(lambda e: e.memset(maskf[0:64, 64:128], 0.0), reads=["maskf"], writes=["maskf"])
        S.dve(lambda e: e.memset(S_f[:], 0.0), writes=["S_f"])
        S.dve(lambda e: e.memset(S_bf[0][:], 0.0), writes=["S_bf0"])
        for i in range(2):
            S.dve(lambda e, i=i: e.memset(qtA[i][:], 0.0), writes=["qtA%d" % i])
            S.dve(lambda e, i=i: e.memset(qtB[i][:], 0.0), writes=["qtB%d" % i])

        for blk in range(8):
            sl = slice(blk * 512, (blk + 1) * 512)
            uT, uk = fe.block(blk)
            for bi, (c0, M) in enumerate(((0, 128), (128, 128), (256, 128), (384, 16))):
                for c in range(16):
                    S.pe(lambda e, uT=uT, c=c, bi=bi, c0=c0, M=M: e.matmul(bk[bi][0:M, :], lhsT=wA[:, c, c0:c0 + M], rhs=uT[:, c, :],
                                                                         start=(c == 0), stop=(c == 15)),
                         reads=[uk, "wA"], writes=["bk%d" % bi])
            S.act(lambda e, sl=sl: e.copy(out=qT[:, sl], in_=bk[0][:]), reads=["bk0"], writes=["qT"])
            S.act(lambda e, sl=sl: e.copy(out=kT[:, sl], in_=bk[1][:]), reads=["bk1"], writes=["kT"])
            S.act(lambda e, sl=sl: e.activation(out=srT[:, sl], in_=bk[2][:], func=AF.Silu), reads=["bk2"], writes=["srT"])
            S.act(lambda e, sl=sl: e.copy(out=zT[:, sl], in_=bk[3][0:16, :]), reads=["bk3"], writes=["zT"])
            for tt in range(4):
                for c in range(16):
                    S.pe(lambda e, uT=uT, c=c, tt=tt: e.matmul(bk[4][:, tt * 128:(tt + 1) * 128], lhsT=uT[:, c, tt * 128:(tt + 1) * 128],
                                                              rhs=wA[:, c, 400:528], start=(c == 0), stop=(c == 15)),
                         reads=[uk, "wA"], writes=["bk4"])
            S.dve(lambda e, blk=blk: e.tensor_copy(out=v[:, blk * 4:(blk + 1) * 4, :], in_=bk[4][:, :].rearrange("p (t n) -> p t n", t=4)),
                  reads=["bk4"], writes=["v"])
        for blk in range(8):
            sl = slice(blk * 512, (blk + 1) * 512)
            pb = bk[blk % 2]
            pk_ = "bk%d" % (blk % 2)
            S.pe(lambda e, sl=sl, pb=pb: e.matmul(pb[:], lhsT=wg2[:, :], rhs=zT[:, sl], start=True, stop=True), reads=["wg2", "zT"], writes=[pk_])
            S.act(lambda e, pb=pb: e.activation(out=tmpe[:], in_=pb[:], func=AF.Exp, scale=-1.0, bias=negb[:]), reads=[pk_, "negb"], writes=["tmpe"])
            S.act(lambda e, sl=sl: e.activation(out=l_all[:, sl], in_=tmpe[:], func=AF.Ln, bias=1.0), reads=["tmpe"], writes=["l_all"])
        S.dve(lambda e: e.tensor_tensor_scan(out=bc[:], data0=mk[:], data1=l_all[:], initial=0.0, op0=ALU.mult, op1=ALU.add),
              reads=["mk", "l_all"], writes=["bc"])
        et = l_all
        bc3 = bc[:, :].rearrange("p (n c) -> p n c", c=64)
        S.act(lambda e: e.activation(out=et[:], in_=bc[:], func=AF.Exp, scale=-1.0 / 16), reads=["bc", "l_all"], writes=["l_all"])
        S.dve(lambda e: e.scalar_tensor_tensor(out=qtT[:], in0=qT[:], scalar=128 ** -0.5, in1=et[:], op0=ALU.mult, op1=ALU.mult),
              reads=["qT", "l_all"], writes=["qtT"])
        S.act(lambda e: e.activation(out=et[:], in_=bc[:], func=AF.Exp, scale=1.0 / 16), reads=["bc", "l_all"], writes=["l_all"])
        S.dve(lambda e: e.tensor_tensor(out=ktT[:], in0=kT[:], in1=et[:], op=ALU.mult), reads=["kT", "l_all"], writes=["ktT"])
        S.dve(lambda e: e.tensor_tensor(out=et[:, :].rearrange("p (n c) -> p n c", c=64), in0=bc3, in1=bc3[:, :, 63:64].to_broadcast([128, 64, 64]),
                                        op=ALU.subtract), reads=["bc", "l_all"], writes=["l_all"])
        S.act(lambda e: e.activation(out=et[:], in_=et[:], func=AF.Exp, scale=1.0 / 16), reads=["l_all"], writes=["l_all"])
        S.dve(lambda e: e.tensor_tensor(out=kendT[:], in0=kT[:], in1=et[:], op=ALU.mult), reads=["kT", "l_all"], writes=["kendT"])
        S.act(lambda e: e.activation(out=dec[:, :], in_=bc3[:, :, 63], func=AF.Exp, scale=-1.0 / 16), reads=["bc"], writes=["dec"])
        b5bf = bk[5][:, :].bitcast(BF16).rearrange("p (j n) -> p j n", n=128)
        for i4 in range(8):
            for j in range(4):
                S.pe(lambda e, i4=i4, j=j: e.transpose(out=b5bf[:, j, :], in_=kendT[:, (i4 * 4 + j) * 128:(i4 * 4 + j + 1) * 128], identity=ident[:]),
                     reads=["kendT", "ident"], writes=["bk5a"])
            S.act(lambda e, i4=i4: e.copy(out=kend_tm[:, i4 * 4:(i4 + 1) * 4, :], in_=b5bf[:, 0:4, :]), reads=["bk5a"], writes=["kend_tm"])
        for i in range(32):
            tl = slice(i * 128, (i + 1) * 128)
            p2 = i % 2
            att, attk = bk[p2], "bk%d" % p2
            po, pok = bk[2 + p2], "bk%d" % (2 + p2)
            sA, sB, sC = (2 * i) % 4, (2 * i + 1) % 4, (2 * i + 2) % 4
            S.pe(lambda e, tl=tl, att=att: e.matmul(att[:, 0:128], lhsT=ktT[:, tl], rhs=qtT[:, tl], start=True, stop=True),
                 reads=["ktT", "qtT"], writes=[attk])
            S.dve(lambda e, att=att, p2=p2: e.tensor_tensor(out=attm[p2][:], in0=att[:, 0:128], in1=maskf[:], op=ALU.mult),
                  reads=[attk, "maskf"], writes=["attm%d" % p2])
            S.act(lambda e, p2=p2, i=i: e.copy(out=qtA[p2][:, 0:64], in_=qtT[:, i * 128:i * 128 + 64]), reads=["qtT"], writes=["qtA%d" % p2])
            S.act(lambda e, p2=p2, i=i: e.copy(out=qtB[p2][:, 64:128], in_=qtT[:, i * 128 + 64:i * 128 + 128]), reads=["qtT"], writes=["qtB%d" % p2])
            S.pe(lambda e, i=i: e.matmul(bk[4][:, 0:128], lhsT=kend_tm[0:64, i, :], rhs=v[0:64, i, :], start=True, stop=True),
                 reads=["kend_tm", "v"], writes=["bk4a"])
            S.dve(lambda e, i=i: e.scalar_tensor_tensor(out=S_f[:], in0=S_f[:], scalar=dec[:, 2 * i:2 * i + 1], in1=bk[4][:, 0:128],
                                                        op0=ALU.mult, op1=ALU.add), reads=["S_f", "dec", "bk4a"], writes=["S_f"])
            S.act(lambda e, sB=sB: e.copy(out=S_bf[sB][:], in_=S_f[:]), reads=["S_f"], writes=["S_bf%d" % sB])
            S.pe(lambda e, po=po, p2=p2, i=i: e.matmul(po[:, 0:128], lhsT=attm[p2][:], rhs=v[:, i, :], start=True, stop=False),
                 reads=["attm%d" % p2, "v"], writes=[pok])
            S.pe(lambda e, po=po, p2=p2, sA=sA: e.matmul(po[:, 0:128], lhsT=qtA[p2][:], rhs=S_bf[sA][:], start=False, stop=False),
                 reads=["qtA%d" % p2, "S_bf%d" % sA], writes=[pok])
            S.pe(lambda e, po=po, p2=p2, sB=sB: e.matmul(po[:, 0:128], lhsT=qtB[p2][:], rhs=S_bf[sB][:], start=False, stop=True),
                 reads=["qtB%d" % p2, "S_bf%d" % sB], writes=[pok])
            S.pe(lambda e, i=i: e.matmul(bk[4][:, 128:256], lhsT=kend_tm[64:128, i, :], rhs=v[64:128, i, :], start=True, stop=True),
                 reads=["kend_tm", "v"], writes=["bk4b"])
            S.dve(lambda e, i=i: e.scalar_tensor_tensor(out=S_f[:], in0=S_f[:], scalar=dec[:, 2 * i + 1:2 * i + 2], in1=bk[4][:, 128:256],
                                                        op0=ALU.mult, op1=ALU.add), reads=["S_f", "dec", "bk4b"], writes=["S_f"])
            S.act(lambda e, sC=sC: e.copy(out=S_bf[sC][:], in_=S_f[:]), reads=["S_f"], writes=["S_bf%d" % sC])
            S.act(lambda e, po=po, p2=p2: e.activation(out=gjunk[:], in_=po[:, 0:128], func=AF.Square, accum_out=gst[:, 2 * p2:2 * p2 + 1]),
                  reads=[pok], writes=["gjunk", "gsa%d" % p2])
            rstd_ops(S, gst[:, 2 * p2:2 * p2 + 1], gst[:, 2 * p2 + 1:2 * p2 + 2], 128, ["gsa%d" % p2], "gsb%d" % p2)
            S.dve(lambda e, po=po, p2=p2: e.tensor_scalar(out=onb[p2][:], in0=po[:, 0:128], scalar1=gst[:, 2 * p2 + 1:2 * p2 + 2], scalar2=None, op0=ALU.mult),
                  reads=[pok, "gsb%d" % p2], writes=["onb%d" % p2])
            S.pe(lambda e, p2=p2: e.transpose(out=b5bf[:, 4 + p2, :], in_=onb[p2][:], identity=ident[:]), reads=["onb%d" % p2, "ident"], writes=["bk5b%d" % p2])
            S.dve(lambda e, p2=p2, tl=tl: e.scalar_tensor_tensor(out=oT_all[:, tl], in0=b5bf[:, 4 + p2, :], scalar=par_sb[:, 1:2], in1=srT[:, tl],
                                                               op0=ALU.mult, op1=ALU.mult),
                  reads=["bk5b%d" % p2, "par", "srT"], writes=["oT_all"])
        S.dma("sp", lambda e: e.dma_start(out=o_out, in_=oT_all[:]), reads=["oT_all"], writes=["o_out"], semkey="o_out")
        S.add("sp", None, reads=["o_out"])
        S.emit(st)
    return nc


def gcols_gla(g):
    base = 1024 + 6 * 256 + 48 + 1024
    q = np.arange(base + g * 128, base + (g + 1) * 128)
    return np.concatenate([q, q + 512, q + 1536, np.arange(base + 2048, base + 2064), q + 1024])


def run_A_gla(h_full, g_pre, w_in, w_gate2, b_gate, gnorm):
    nc = _get("A_gla", build_A_gla)
    in_maps = []
    for k in range(NCORES):
        b, g = k // 4, k % 4
        ch = slice(g * 128, (g + 1) * 128)
        par = np.stack([b_gate[ch], gnorm[ch]], axis=1).astype(np.float32)
        in_maps.append({"h_b": np.ascontiguousarray(h_full[b]), "g_pre": g_pre,
                        "w_a": np.ascontiguousarray(w_in[:, gcols_gla(g)]), "par": np.ascontiguousarray(par),
                        "wg2": np.ascontiguousarray(w_gate2[:, ch])})
    res = run_bass_kernel_spmd(nc, in_maps, core_ids=list(range(NCORES)))
    out = np.zeros((512, 2 * T), dtype=ml_dtypes.bfloat16)
    for k in range(NCORES):
        b, g = k // 4, k % 4
        out[g * 128:(g + 1) * 128, b * T:(b + 1) * T] = res.results[k]["o_out"]
    return out


NEGB = -30000.0


def nsa_tables(g):
    slopes = np.exp2(-8.0 * np.arange(1, 17, dtype=np.float64) / 16)[4 * g:4 * g + 4]
    tq = np.arange(T, dtype=np.float64).reshape(32, 1, 128)
    x = (-slopes.reshape(1, 4, 1) * tq).astype(np.float32)
    hi = x.astype(ml_dtypes.bfloat16).astype(np.float32)
    lo = x - hi
    qtab = np.stack([hi, np.broadcast_to((64 * slopes).reshape(1, 4, 1), x.shape).astype(np.float32),
                     np.broadcast_to(slopes.reshape(1, 4, 1), x.shape).astype(np.float32), lo]).astype(np.float32)
    t = np.arange(T)
    ktab = np.stack([np.ones(T), t // 64, t % 64, np.ones(T)]).astype(np.float32)
    tk = 16 * np.arange(256) + 31
    ctab = np.stack([np.ones(256), tk // 64, tk % 64, np.ones(256)]).astype(np.float32)
    p = np.arange(128).reshape(128, 1)
    qq = np.arange(128).reshape(1, 128)
    cb = np.zeros((128, 33, 128), np.float32)
    for idx in range(33):
        i, nt = (idx, 0) if idx < 17 else (idx - 1, 1)
        n = nt * 128 + p
        valid = (128 * i + qq >= 16 * n + 31) & (n <= 254)
        cb[:, idx, :] = np.where(valid, 0.0, NEGB)
    causal = np.where(p <= qq, 0.0, NEGB).astype(np.float32)
    anti = np.where(p > qq, 0.0, NEGB).astype(np.float32)
    jj = np.arange(128).reshape(1, 128)
    Dm = jj - 64 - (p >= 64)
    keepM = (Dm <= -2).astype(np.float32)
    addM = np.where(Dm > 0, -1.0, np.where(Dm >= -1, 1e4, 0.0)).astype(np.float32)
    E = (np.arange(T).reshape(1, T) // 64 == np.arange(64).reshape(64, 1)).astype(np.float32)
    n = np.arange(256).reshape(256, 1)
    j = np.arange(64).reshape(1, 64)
    OV = ((16 * n < 64 * j + 64) & (64 * j < 16 * n + 32) & (n <= 254)).astype(np.float32)
    return dict(qtab=np.ascontiguousarray(qtab), ktab=ktab, ctab=ctab, cbias=cb, cmask=np.stack([causal, anti]),
                smask=np.stack([keepM, addM]), Emat=E, OV=np.ascontiguousarray(OV.reshape(2, 128, 64)))


def build_A_nsa(stop=99):
    import os
    DBG = os.environ.get('NSA_DBG', '')
    nc = bass.Bass("TRN2", target_bir_lowering=False)
    dt_in = lambda name, shape: nc.dram_tensor(name, shape, F32, kind="ExternalInput").ap()
    h_b = dt_in("h_b", [T, D])
    g_pre = dt_in("g_pre", [D])
    w_a = dt_in("w_a", [D, 652])
    cmpw = dt_in("cmpw", [2, 64, 2048])
    peT = dt_in("peT", [2, 64, 32])
    qtab = dt_in("qtab", [4, 32, 4, 128])
    ktab = dt_in("ktab", [4, T])
    ctab = dt_in("ctab", [4, 256])
    cbias_d = dt_in("cbias", [128, 33, 128])
    cmask_d = dt_in("cmask", [2, 128, 128])
    smask_d = dt_in("smask", [2, 128, 128])
    E_d = dt_in("Emat", [64, T])
    OV_d = dt_in("OV", [2, 128, 64])
    o_out = nc.dram_tensor("o_out", [256, T], BF16, kind="ExternalOutput").ap()
    with contextlib.ExitStack() as st:
        sb = lambda name, shape, dt: st.enter_context(nc.sbuf_tensor(name, shape, dt))
        ps = lambda name, shape, dt: st.enter_context(nc.psum_tensor(name, shape, dt))
        S = Sched(nc)
        ident, _ = make_identity(S, nc, sb)
        fe = FrontEnd(S, nc, sb, ps, h_b, g_pre, ident, nbuf=1)
        wA = load_wA(S, sb, w_a, 652)
        bk = [ps("bk%d" % i, [128, 512], F32) for i in range(6)]

        QaT = sb("QaT", [128, 32, 4, 128], BF16)
        kcT = sb("kcT", [64, T], BF16)
        vcT = sb("vcT", [64, T], BF16)
        ksT = sb("ksT", [128, T], BF16)
        kwT = sb("kwT", [128, T], BF16)
        vs_aug = sb("vs_aug", [128, 32, 66], BF16)
        vw_aug = sb("vw_aug", [128, 32, 66], BF16)
        gates_sb = sb("gates_sb", [128, 32, 12], F32)
        Wc = sb("Wc", [64, 2, 2048], BF16)
        peT_sb = sb("peT_sb", [64, 2, 32], BF16)
        kcmpT = sb("kcmpT", [128, 256], BF16)
        vcmp_aug = sb("vcmp_aug", [128, 2, 130], BF16)
        cbias = sb("cbias_sb", [128, 33, 128], BF16)
        cmask = sb("cmask_sb", [128, 2, 128], BF16)
        smask = sb("smask_sb", [128, 2, 128], F32)
        Emat = sb("Emat_sb", [64, T], BF16)
        selbT_all = sb("selbT_all", [64, 32, 128], BF16)
        PT = [sb("PT%d" % i, [128, 512], BF16) for i in range(3)]
        sm = [sb("sm%d" % i, [128, 512], F32) for i in range(2)]
        selx = [sb("selx%d" % i, [64, 512], BF16) for i in range(2)]
        constk = sb("constk", [64, 1], F32)
        constv = sb("constv", [1, 64], BF16)
        ones_row = sb("ones_row", [1, 128], BF16)
        small = sb("small", [128, 64], F32)
        imp = sb("imp", [128, 64], F32)
        score = sb("score", [128, 64], F32)
        sc2 = sb("sc2", [128, 64], F32)
        m8 = sb("m8", [128, 16], F32)
        selb = sb("selb", [128, 64], BF16)
        onsa = sb("onsa", [128, 256], BF16)
        ocmp_v = [fe.h_t[j][:, :].bitcast(BF16).rearrange("p (i n) -> p i n", n=256) for j in range(2)]
        oT_all = fe.uT[0][:, :, :].rearrange("p c n -> p (c n)").rearrange("p (c n) -> p c n", c=2)

        S.dma("pool", lambda e: e.dma_start(out=QaT[64:68, :, :, :], in_=qtab), writes=["QaTaug"], semkey="qtab")
        S.dma("pool", lambda e: e.dma_start(out=ksT[64:68, :], in_=ktab), writes=["ksTaug"], semkey="ktab1")
        S.dma("pool", lambda e: e.dma_start(out=kwT[64:68, :], in_=ktab), writes=["kwTaug"], semkey="ktab2")
        S.dma("pool", lambda e: e.dma_start(out=kcmpT[64:68, :], in_=ctab), writes=["kcmpTaug"], semkey="ctab")
        S.dma("pool", lambda e: e.dma_start(out=cbias[:], in_=cbias_d), writes=["cbias"], semkey="cbias")
        S.dma("pool", lambda e: e.dma_start(out=cmask[:], in_=cmask_d.rearrange("m p n -> p m n")), writes=["cmask"], semkey="cmask")
        S.dma("sp", lambda e: e.dma_start(out=smask[:], in_=smask_d.rearrange("m p n -> p m n")), writes=["smask"], semkey="smask")
        S.dma("pool", lambda e: e.dma_start(out=Emat[:], in_=E_d), writes=["Emat"], semkey="Emat")
        S.dma("pool", lambda e: e.dma_start(out=Wc[:], in_=cmpw.rearrange("k d n -> d k n")), writes=["Wc"], semkey="Wc")
        S.dma("pool", lambda e: e.dma_start(out=peT_sb[:], in_=peT.rearrange("k d l -> d k l")), writes=["peT"], semkey="peT")
        S.dve(lambda e: e.memset(vcmp_aug[:], 0.0), writes=["vcmp_aug"])
        S.dma("pool", lambda e: e.dma_start(out=vcmp_aug[:, :, 65:129], in_=OV_d.rearrange("t p j -> p t j")), reads=["vcmp_aug"], writes=["vcmp_aug"], semkey="OV")
        S.dve(lambda e: e.memset(vcmp_aug[:, :, 64:65], 1.0), reads=["vcmp_aug"], writes=["vcmp_aug"])
        S.dve(lambda e: e.memset(vs_aug[:, :, 64:65], 1.0), writes=["vs_one"])
        S.dve(lambda e: e.memset(vw_aug[:, :, 64:65], 1.0), writes=["vw_one"])
        S.dve(lambda e: e.memset(ones_row[:], 1.0), writes=["ones_row"])
        S.dve(lambda e: e.memset(kcmpT[0:64, :], 0.0), writes=["kcmpT"])

        def finish():
            for c in range(2):
                S.dma("sp", lambda e, c=c: e.dma_start(out=o_out[c * 128:(c + 1) * 128, :], in_=oT_all[:, c, :]), reads=["fe_uT0"], writes=["o_out%d" % c], semkey="o_out%d" % c)
            S.add("sp", None, reads=["o_out0", "o_out1"])
            S.emit(st)
            return nc

        if stop == 0:
            return finish()
        fm_i = [0]
        for blk in range(8):
            sl = slice(blk * 512, (blk + 1) * 512)
            uT, uk = fe.block(blk)
            for ci in range(8):
                if 'noFM' in DBG or ('noQ' in DBG and ci < 4) or ('noK' in DBG and ci >= 4):
                    continue
                bi = fm_i[0] % 4
                fm_i[0] += 1
                pb, pk_ = bk[bi], "bk%d" % bi
                for c in range(16):
                    S.pe(lambda e, uT=uT, c=c, ci=ci, pb=pb: e.matmul(pb[0:64, :], lhsT=wA[:, c, ci * 64:(ci + 1) * 64], rhs=uT[:, c, :],
                                                                    start=(c == 0), stop=(c == 15)),
                         reads=[uk, "wA"], writes=[pk_])
                if ci < 4:
                    S.act(lambda e, pb=pb, ci=ci, blk=blk: e.mul(QaT[0:64, blk * 4:(blk + 1) * 4, ci, :], pb[0:64, :].rearrange("p (t n) -> p t n", t=4), 0.125),
                          reads=[pk_], writes=["QaT"])
                else:
                    dst, dk_ = ((kcT, "kcT"), (vcT, "vcT"), (ksT, "ksT"), (kwT, "kwT"))[ci - 4]
                    S.dve(lambda e, pb=pb, dst=dst, sl=sl: e.tensor_copy(out=dst[0:64, sl], in_=pb[0:64, :]), reads=[pk_], writes=[dk_])
            for tt in range(4):
                if 'noTM' in DBG:
                    continue
                tile = blk * 4 + tt
                for c in range(16):
                    S.pe(lambda e, uT=uT, c=c, tt=tt: e.matmul(bk[4][:, 0:128], lhsT=uT[:, c, tt * 128:(tt + 1) * 128], rhs=wA[:, c, 512:640],
                                                              start=(c == 0), stop=(c == 15)),
                         reads=[uk, "wA"], writes=["bk4"])
                if 'noGate' not in DBG:
                    for c in range(16):
                        S.pe(lambda e, uT=uT, c=c, tt=tt: e.matmul(bk[5][:, 0:12], lhsT=uT[:, c, tt * 128:(tt + 1) * 128], rhs=wA[:, c, 640:652],
                                                                  start=(c == 0), stop=(c == 15)),
                             reads=[uk, "wA"], writes=["bk5"])
                S.dve(lambda e, tile=tile: e.tensor_copy(out=vs_aug[:, tile, 0:64], in_=bk[4][:, 0:64]), reads=["bk4"], writes=["vs_aug"])
                S.dve(lambda e, tile=tile: e.tensor_copy(out=vw_aug[:, tile, 0:64], in_=bk[4][:, 64:128]), reads=["bk4"], writes=["vw_aug"])
                if 'noGate' not in DBG:
                    S.act(lambda e, tile=tile: e.activation(out=gates_sb[:, tile, :], in_=bk[5][:, 0:12], func=AF.Sigmoid), reads=["bk5"], writes=["gates"])

        if stop == 1:
            return finish()
        for l in range(32):
            S.pe(lambda e, l=l: e.matmul(bk[2][0:64, 0:1], lhsT=Wc[:, 0, l * 64:(l + 1) * 64], rhs=peT_sb[:, 0, l:l + 1], start=(l == 0), stop=(l == 31)),
                 reads=["Wc", "peT"], writes=["bk2"])
        S.dve(lambda e: e.tensor_copy(out=constk[:], in_=bk[2][0:64, 0:1]), reads=["bk2"], writes=["constk"])
        for l in range(32):
            S.pe(lambda e, l=l: e.matmul(bk[3][0:1, 0:64], lhsT=peT_sb[:, 1, l:l + 1], rhs=Wc[:, 1, l * 64:(l + 1) * 64], start=(l == 0), stop=(l == 31)),
                 reads=["Wc", "peT"], writes=["bk3"])
        S.dve(lambda e: e.tensor_copy(out=constv[:], in_=bk[3][0:1, 0:64]), reads=["bk3"], writes=["constv"])
        for l in range(32):
            S.pe(lambda e, l=l: e.matmul(bk[0][0:64, 0:255], lhsT=Wc[:, 0, l * 64:(l + 1) * 64], rhs=kcT[:, l:l + 16 * 254 + 1:16], start=(l == 0), stop=(l == 31)),
                 reads=["Wc", "kcT"], writes=["bk0"])
        S.dve(lambda e: e.tensor_scalar(out=kcmpT[0:64, 0:255], in0=bk[0][0:64, 0:255], scalar1=constk[:, 0:1], scalar2=None, op0=ALU.add),
              reads=["bk0", "constk", "kcmpT"], writes=["kcmpT"])
        for nt in range(2):
            M = 128 if nt == 0 else 127
            n0 = nt * 128
            for l in range(32):
                S.pe(lambda e, l=l, nt=nt, M=M, n0=n0: e.matmul(bk[1][0:M, nt * 64:(nt + 1) * 64], lhsT=vcT[:, l + 16 * n0:l + 16 * (n0 + M - 1) + 1:16],
                                                              rhs=Wc[:, 1, l * 64:(l + 1) * 64], start=(l == 0), stop=False),
                     reads=["Wc", "vcT"], writes=["bk1"])
            S.pe(lambda e, nt=nt, M=M: e.matmul(bk[1][0:M, nt * 64:(nt + 1) * 64], lhsT=ones_row[0:1, 0:M], rhs=constv[0:1, :], start=False, stop=True),
                 reads=["ones_row", "constv"], writes=["bk1"])
            S.dve(lambda e, nt=nt, M=M: e.tensor_copy(out=vcmp_aug[0:M, nt, 0:64], in_=bk[1][0:M, nt * 64:(nt + 1) * 64]),
                  reads=["bk1", "vcmp_aug"], writes=["vcmp_aug"])

        if stop == 3:
            return finish()
        QK = ["QaT", "QaTaug"]
        cnt = {"st": 0, "pt": 0, "sm": 0}

        def score_tile(klhsT, kkeys, i, extra=None, bias=None):
            si = cnt["st"] % 3
            cnt["st"] += 1
            pb, pk_ = bk[si], "bk%d" % si
            S.pe(lambda e: e.matmul(pb[:, :], lhsT=klhsT, rhs=QaT[0:68, i, :, :].rearrange("p h n -> p (h n)"), start=True, stop=(extra is None)),
                 reads=kkeys + QK, writes=[pk_])
            if extra is not None:
                elhsT, erhs, ekeys = extra
                S.pe(lambda e: e.matmul(pb[:, :], lhsT=elhsT, rhs=erhs, start=False, stop=True), reads=ekeys, writes=[pk_])
            pi = cnt["pt"] % 3
            cnt["pt"] += 1
            pt, ptk = PT[pi], "PT%d" % pi
            if bias is not None:
                bap, bkey = bias
                mi = cnt["sm"] % 2
                cnt["sm"] += 1
                smt, smk = sm[mi], "sm%d" % mi
                S.dve(lambda e: e.tensor_tensor(out=smt[:, :].rearrange("p (h n) -> p h n", h=4), in0=pb[:, :].rearrange("p (h n) -> p h n", h=4),
                                                in1=bap, op=ALU.add), reads=[pk_, bkey], writes=[smk])
                S.act(lambda e: e.activation(out=pt[:], in_=smt[:], func=AF.Exp), reads=[smk], writes=[ptk])
            else:
                S.act(lambda e: e.activation(out=pt[:], in_=pb[:], func=AF.Exp), reads=[pk_], writes=[ptk])
            return pt, ptk

        b5bf = bk[5][:, :].bitcast(BF16)
        for i in range(32):
            nts = [0] if i < 16 else [0, 1]
            first = True
            for nt in nts:
                if nt == 0:
                    bias = (cbias[:, i:i + 1, :].to_broadcast([128, 4, 128]), "cbias") if i <= 16 else None
                else:
                    bias = (cbias[:, i + 1:i + 2, :].to_broadcast([128, 4, 128]), "cbias")
                pt, ptk = score_tile(kcmpT[0:68, nt * 128:(nt + 1) * 128], ["kcmpT", "kcmpTaug"], i, bias=bias)
                for h in range(4):
                    pb = bk[3 + h // 2]
                    S.pe(lambda e, pt=pt, h=h, nt=nt, pb=pb, st_=(first and h % 2 == 0), sp_=(nt == nts[-1]):
                         e.matmul(pb[:, (h % 2) * 129:(h % 2) * 129 + 129], lhsT=pt[:, h * 128:(h + 1) * 128], rhs=vcmp_aug[:, nt, 0:129],
                                  start=st_, stop=sp_, skip_group_check=True),
                         reads=[ptk, "vcmp_aug"], writes=["bk%d" % (3 + h // 2)])
                first = False
            for hb in range(2):
                S.dve(lambda e, hb=hb: e.tensor_scalar(out=small[:, 2 * hb:2 * hb + 2], in0=bk[3 + hb][:, 0:258].rearrange("p (h n) -> p h n", h=2)[:, :, 64],
                                                       scalar1=1e-30, scalar2=None, op0=ALU.max), reads=["bk%d" % (3 + hb)], writes=["small_a"])
            S.dve(lambda e: e.reciprocal(out=small[:, 0:4], in_=small[:, 0:4]), reads=["small_a"], writes=["small_a"])
            for h in range(4):
                src = bk[3 + h // 2][:, (h % 2) * 129 + 65:(h % 2) * 129 + 129]
                if h == 0:
                    S.dve(lambda e, src=src: e.tensor_scalar(out=imp[:], in0=src, scalar1=small[:, 0:1], scalar2=None, op0=ALU.mult),
                          reads=["bk3", "small_a"], writes=["imp"])
                else:
                    S.dve(lambda e, src=src, h=h: e.scalar_tensor_tensor(out=imp[:], in0=src, scalar=small[:, h:h + 1], in1=imp[:], op0=ALU.mult, op1=ALU.add),
                          reads=["bk%d" % (3 + h // 2), "small_a", "imp"], writes=["imp"])
            S.dve(lambda e, i=i: e.tensor_tensor(out=small[:, 4:8], in0=small[:, 0:4], in1=gates_sb[:, i, 0:12:3], op=ALU.mult),
                  reads=["small_a", "gates"], writes=["small_b"])
            ov_, ok_ = ocmp_v[i // 16], "fe_h%d" % (i // 16)
            for h in range(4):
                src = bk[3 + h // 2][:, (h % 2) * 129:(h % 2) * 129 + 64]
                S.dve(lambda e, src=src, h=h, ov_=ov_, i=i: e.tensor_scalar(out=ov_[:, i % 16, h * 64:(h + 1) * 64], in0=src, scalar1=small[:, 4 + h:5 + h],
                                                                          scalar2=None, op0=ALU.mult),
                      reads=["bk%d" % (3 + h // 2), "small_b"], writes=[ok_])
            c0 = 64 - 2 * i
            S.dve(lambda e, c0=c0: e.tensor_tensor(out=score[:], in0=imp[:], in1=smask[:, 0, c0:c0 + 64], op=ALU.mult), reads=["imp", "smask"], writes=["score"])
            S.dve(lambda e, c0=c0: e.tensor_tensor(out=score[:], in0=score[:], in1=smask[:, 1, c0:c0 + 64], op=ALU.add), reads=["score", "smask"], writes=["score"])
            S.dve(lambda e: e.memset(score[:, 0:1], 1e4), reads=["score"], writes=["score"])
            S.dve(lambda e: e.max(out=m8[:, 0:8], in_=score[:]), reads=["score"], writes=["m8a"])
            S.dve(lambda e: e.match_replace(out=sc2[:], in_to_replace=m8[:, 0:8], in_values=score[:], imm_value=-1e30), reads=["score", "m8a"], writes=["sc2"])
            S.dve(lambda e: e.max(out=m8[:, 8:16], in_=sc2[:]), reads=["sc2"], writes=["m8b"])
            S.dve(lambda e: e.tensor_scalar(out=sc2[:], in0=score[:], scalar1=m8[:, 15:16], scalar2=None, op0=ALU.is_ge), reads=["score", "m8b", "sc2"], writes=["sc2"])
            S.dve(lambda e: e.tensor_scalar(out=selb[:], in0=sc2[:], scalar1=-1.0, scalar2=-NEGB, op0=ALU.add, op1=ALU.mult), reads=["sc2"], writes=["selb"])
            S.pe(lambda e: e.transpose(out=b5bf[0:64, 0:128], in_=selb[:], identity=ident[:]), reads=["selb", "ident"], writes=["bk5"])
            S.act(lambda e, i=i: e.copy(out=selbT_all[:, i, :], in_=b5bf[0:64, 0:128]), reads=["bk5"], writes=["selbT"])

        if stop == 4:
            return finish()
        work = []
        for i in range(32):
            for j in range(max(0, i - 4), i + 1):
                work.append(("w", i, j))
            for j in range(0, i + 1):
                work.append(("s", i, j))
        pend = []

        def emit_qk(wi):
            kind, i, j = work[wi]
            ks_ = slice(j * 128, (j + 1) * 128)
            if kind == "w":
                bias = None
                if j == i:
                    bias = (cmask[:, 0:1, :].to_broadcast([128, 4, 128]), "cmask")
                elif j == i - 4:
                    bias = (cmask[:, 1:2, :].to_broadcast([128, 4, 128]), "cmask")
                return score_tile(kwT[0:68, ks_], ["kwT", "kwTaug"], i, bias=bias)
            p2 = i % 2
            if j == 0:
                S.act(lambda e, i=i, p2=p2: e.copy(out=selx[p2][:, :].rearrange("p (h n) -> p h n", h=4), in_=selbT_all[:, i:i + 1, :].to_broadcast([64, 4, 128])),
                      reads=["selbT"], writes=["selx%d" % p2])
            bias = (cmask[:, 0:1, :].to_broadcast([128, 4, 128]), "cmask") if j == i else None
            return score_tile(ksT[0:68, ks_], ["ksT", "ksTaug"], i, extra=(Emat[:, ks_], selx[p2][:, :], ["Emat", "selx%d" % p2]), bias=bias)

        def emit_pv(wi, pt, ptk):
            kind, i, j = work[wi]
            if kind == "w":
                pb, pk_, vaug, vk, j0, ones_k = bk[3], "bk3", vw_aug, "vw_aug", max(0, i - 4), "vw_one"
            else:
                pb, pk_, vaug, vk, j0, ones_k = bk[4], "bk4", vs_aug, "vs_aug", 0, "vs_one"
            for h in range(4):
                S.pe(lambda e, h=h: e.matmul(pb[:, h * 65:(h + 1) * 65], lhsT=pt[:, h * 128:(h + 1) * 128], rhs=vaug[:, j, 0:65],
                                             start=(j == j0 and h == 0), stop=(j == i), skip_group_check=True),
                     reads=[ptk, vk, ones_k], writes=[pk_])
            if kind == "s" and j == i:
                emit_combine(i)

        def emit_combine(i):
            ov_, ok_ = ocmp_v[i // 16], "fe_h%d" % (i // 16)
            for bi, (pb, pk_, gcol) in enumerate(((bk[3], "bk3", 2), (bk[4], "bk4", 1))):
                o0 = 8 + 8 * bi
                S.dve(lambda e, pb=pb, o0=o0: e.tensor_scalar(out=small[:, o0:o0 + 4], in0=pb[:, 0:260].rearrange("p (h n) -> p h n", h=4)[:, :, 64],
                                                            scalar1=1e-30, scalar2=None, op0=ALU.max), reads=[pk_], writes=["small_c%d" % bi])
                S.dve(lambda e, o0=o0: e.reciprocal(out=small[:, o0:o0 + 4], in_=small[:, o0:o0 + 4]), reads=["small_c%d" % bi], writes=["small_c%d" % bi])
                S.dve(lambda e, o0=o0, gcol=gcol, i=i: e.tensor_tensor(out=small[:, o0 + 4:o0 + 8], in0=small[:, o0:o0 + 4], in1=gates_sb[:, i, gcol:12:3], op=ALU.mult),
                      reads=["small_c%d" % bi, "gates"], writes=["small_d%d" % bi])
            for h in range(4):
                hs = slice(h * 64, (h + 1) * 64)
                S.dve(lambda e, h=h, hs=hs, ov_=ov_, i=i: e.scalar_tensor_tensor(out=onsa[:, hs], in0=bk[3][:, h * 65:h * 65 + 64], scalar=small[:, 12 + h:13 + h],
                                                                               in1=ov_[:, i % 16, hs], op0=ALU.mult, op1=ALU.add),
                      reads=["bk3", "small_d0", ok_, "onsa"], writes=["onsa"])
                S.dve(lambda e, h=h, hs=hs: e.scalar_tensor_tensor(out=onsa[:, hs], in0=bk[4][:, h * 65:h * 65 + 64], scalar=small[:, 20 + h:21 + h],
                                                                 in1=onsa[:, hs], op0=ALU.mult, op1=ALU.add),
                      reads=["bk4", "small_d1", "onsa"], writes=["onsa"])
            for c in range(2):
                S.pe(lambda e, c=c: e.transpose(out=b5bf[:, 128 + c * 128:256 + c * 128], in_=onsa[:, c * 128:(c + 1) * 128], identity=ident[:]),
                     reads=["onsa", "ident"], writes=["bk5"])
            S.act(lambda e, i=i: e.copy(out=oT_all[:, :, i * 128:(i + 1) * 128], in_=b5bf[:, 128:384].rearrange("p (c n) -> p c n", c=2)),
                  reads=["bk5"], writes=["fe_uT0"])

        LOOK = 2
        res = {}
        nW = len(work)
        for wi in range(nW + LOOK):
            if wi < nW:
                res[wi] = emit_qk(wi)
            if wi - LOOK >= 0:
                pt, ptk = res.pop(wi - LOOK)
                emit_pv(wi - LOOK, pt, ptk)
        return finish()


def gcols_nsa(g):
    q = np.arange(g * 256, (g + 1) * 256)
    kv = lambda n: np.arange(1024 + n * 256 + g * 64, 1024 + n * 256 + (g + 1) * 64)
    gates = np.arange(2560 + g * 12, 2560 + (g + 1) * 12)
    return np.concatenate([q, kv(0), kv(1), kv(2), kv(4), kv(3), kv(5), gates])


def run_A_nsa(h_full, g_pre, w_in, cmp_w, cmp_pe):
    import os
    stop = int(os.environ.get("NSA_STOP", "99"))
    nc = _get("A_nsa", lambda: build_A_nsa(stop))
    cmpw = np.ascontiguousarray(cmp_w.reshape(2, 32, 64, 64).transpose(0, 2, 1, 3).reshape(2, 64, 2048))
    peT = np.ascontiguousarray(cmp_pe.transpose(0, 2, 1))
    in_maps = []
    for k in range(NCORES):
        b, g = k // 4, k % 4
        m = {"h_b": np.ascontiguousarray(h_full[b]), "g_pre": g_pre, "w_a": np.ascontiguousarray(w_in[:, gcols_nsa(g)]),
             "cmpw": cmpw, "peT": peT}
        m.update(_get(("tab", g), lambda g=g: nsa_tables(g)))
        in_maps.append(m)
    res = run_bass_kernel_spmd(nc, in_maps, core_ids=list(range(NCORES)))
    out = np.zeros((1024, 2 * T), dtype=ml_dtypes.bfloat16)
    for k in range(NCORES):
        b, g = k // 4, k % 4
        out[g * 256:(g + 1) * 256, b * T:(b + 1) * T] = res.results[k]["o_out"]
    return out


def layer_forward(h, p_l, W, l):
    g_pre = np.ascontiguousarray(W["norm_mix_pre"][l])
    w_in = W["w_in"][l]
    oa = run_A_nsa(h, g_pre, w_in, W["nsa_cmp_w"][l], W["nsa_cmp_pe"][l])
    ob = run_A_lru(h, g_pre, w_in, W["lru_conv_w"][l], W["lru_conv_b"][l], W["lru_wa"][l], W["lru_ba"][l],
                   W["lru_wi"][l], W["lru_bi"][l], W["lru_lambda"][l])
    oc = run_A_gla(h, g_pre, w_in, W["gla_w_gate2"][l], W["gla_b_gate"][l], W["gla_norm"][l])
    mixT = np.concatenate([oa, ob, oc], axis=0)
    gains = np.ascontiguousarray(np.stack([W["norm_mix_post"][l], W["norm_mlp_pre"][l], W["norm_mlp_post"][l]]))
    h2 = run_B(mixT, h.reshape(2 * T, D), p_l, np.ascontiguousarray(W["w_out"][l]), gains, np.ascontiguousarray(W["w_up"][l]),
               np.ascontiguousarray(W["w_down"][l]), np.ascontiguousarray(W["w_ple_gate"][l]), np.ascontiguousarray(W["w_ple"][l]))
    return h2.reshape(2, T, D)


def kernel(**inputs):
    W = {k: np.asarray(v, dtype=np.float32) for k, v in inputs.items()}
    h = np.ascontiguousarray(W["x"])
    for l in range(2):
        h = layer_forward(h, W["p"][l], W, l)
    return np.ascontiguousarray(h.astype(np.float32))
```
